# Optimizing a Trainium2 kernel written in Bass

```python
import math
import jax, jax.numpy as jnp
from jax import lax
import numpy as np

D_MODEL = 1024
BATCH = 32
SEQ = 2048
DEPTH = 1

MIX_WIDTH = D_MODEL
POOL_WIDTH = MIX_WIDTH // 2
POOL_WINDOWS = (2, 4, 8, 16)
POOL_GROUPS = len(POOL_WINDOWS)
POOL_CH = POOL_WIDTH // POOL_GROUPS
SGU_WIDTH = MIX_WIDTH - POOL_WIDTH
SGU_HEADS = 4
SGU_HD = SGU_WIDTH // SGU_HEADS
CHUNK = 128
IN_COLS = POOL_WIDTH + 2 * SGU_WIDTH
D_FF = int(math.ceil((8 * D_MODEL / 3) / 256) * 256)
LN_EPS = 1e-5
DEEPNORM_ALPHA = float((2.0 * DEPTH) ** 0.25)
DEEPNORM_BETA = float((8.0 * DEPTH) ** -0.25)

kernel_name = "hybrid_pool_sgu_deepnorm_layer"


def layer_norm(x, g, b):
    xf = x.astype(jnp.float32)
    mu = jnp.mean(xf, axis=-1, keepdims=True)
    var = jnp.mean(jnp.square(xf - mu), axis=-1, keepdims=True)
    out = (xf - mu) * lax.rsqrt(var + LN_EPS)
    return (out * g.astype(jnp.float32) + b.astype(jnp.float32)).astype(x.dtype)


def causal_multiscale_pool(xp):
    S = xp.shape[1]
    xf = xp.astype(jnp.float32)
    cs0 = jnp.pad(jnp.cumsum(xf, axis=1), ((0, 0), (1, 0), (0, 0)))
    pos = jnp.arange(1, S + 1, dtype=jnp.int32)
    outs = []
    for g, w in enumerate(POOL_WINDOWS):
        sl = slice(g * POOL_CH, (g + 1) * POOL_CH)
        c = cs0[..., sl]
        lower = jnp.pad(c, ((0, 0), (w - 1, 0), (0, 0)))[:, :S]
        cnt = jnp.minimum(pos, w).astype(jnp.float32)[None, :, None]
        outs.append((c[:, 1:] - lower) / cnt - xf[..., sl])
    return jnp.stack(outs, axis=2)


def spatial_gating(z, ln_g, ln_b, w_s, b_s):
    B, S, _ = z.shape
    u, v = z[..., :SGU_WIDTH], z[..., SGU_WIDTH:]
    v = layer_norm(v, ln_g, ln_b)
    v = v.reshape(B, S // CHUNK, CHUNK, SGU_HEADS, SGU_HD)
    mask = jnp.tril(jnp.ones((CHUNK, CHUNK), dtype=w_s.dtype))
    ws = w_s * mask[None]
    mixed = jnp.einsum('hts,bnshd->bnthd', ws, v)
    mixed = mixed + jnp.transpose(b_s)[None, None, :, :, None]
    return u * mixed.reshape(B, S, SGU_WIDTH)


def swiglu_ffn(h, w_gate_up, w_down):
    gu = jnp.einsum('bsd,df->bsf', h, w_gate_up)
    gate, up = gu[..., :D_FF], gu[..., D_FF:]
    return jnp.einsum('bsf,fd->bsd', jax.nn.silu(gate) * up, w_down)


def setup_inputs(seed: int = 0) -> dict:
    key = jax.random.key(seed)
    ks = jax.random.split(key, 16)
    f32 = jnp.float32
    def nrm(k, shape, scale):
        return jax.random.normal(k, shape, f32) * scale
    return {
        "x": jax.random.normal(ks[0], (BATCH, SEQ, D_MODEL), f32),
        "w_in": nrm(ks[1], (DEPTH, D_MODEL, IN_COLS), D_MODEL ** -0.5),
        "pool_w": nrm(ks[2], (DEPTH, POOL_GROUPS, POOL_CH, POOL_CH), POOL_CH ** -0.5),
        "pool_scale": 1.0 + nrm(ks[3], (DEPTH, POOL_WIDTH), 0.1),
        "sgu_ln_g": 1.0 + nrm(ks[4], (DEPTH, SGU_WIDTH), 0.01),
        "sgu_ln_b": nrm(ks[5], (DEPTH, SGU_WIDTH), 0.01),
        "sgu_w": nrm(ks[6], (DEPTH, SGU_HEADS, CHUNK, CHUNK), CHUNK ** -0.5),
        "sgu_b": 1.0 + nrm(ks[7], (DEPTH, SGU_HEADS, CHUNK), 0.01),
        "w_out": nrm(ks[8], (DEPTH, MIX_WIDTH, D_MODEL), MIX_WIDTH ** -0.5 * DEEPNORM_BETA),
        "ln1_g": 1.0 + nrm(ks[9], (DEPTH, D_MODEL), 0.01),
        "ln1_b": nrm(ks[10], (DEPTH, D_MODEL), 0.01),
        "w_gate_up": nrm(ks[11], (DEPTH, D_MODEL, 2 * D_FF), D_MODEL ** -0.5),
        "w_down": nrm(ks[12], (DEPTH, D_FF, D_MODEL), D_FF ** -0.5 * DEEPNORM_BETA),
        "ln2_g": 1.0 + nrm(ks[13], (DEPTH, D_MODEL), 0.01),
        "ln2_b": nrm(ks[14], (DEPTH, D_MODEL), 0.01),
    }


def reference(x, w_in, pool_w, pool_scale, sgu_ln_g, sgu_ln_b, sgu_w, sgu_b,
              w_out, ln1_g, ln1_b, w_gate_up, w_down, ln2_g, ln2_b):
    B, S, _ = x.shape
    alpha = jnp.asarray(DEEPNORM_ALPHA, dtype=x.dtype)
    for l in range(DEPTH):
        proj = jnp.einsum('bsd,dc->bsc', x, w_in[l])
        xp = proj[..., :POOL_WIDTH]
        zg = jax.nn.gelu(proj[..., POOL_WIDTH:], approximate=False)
        pooled = causal_multiscale_pool(xp)
        pool_out = jnp.einsum('bsgc,gcd->bsgd', pooled, pool_w[l].astype(jnp.float32))
        pool_out = (pool_out.reshape(B, S, POOL_WIDTH) * pool_scale[l]).astype(x.dtype)
        sgu_out = spatial_gating(zg, sgu_ln_g[l], sgu_ln_b[l], sgu_w[l], sgu_b[l])
        mix = jnp.concatenate([pool_out, sgu_out], axis=-1)
        mix = jnp.einsum('bsc,cd->bsd', mix, w_out[l])
        h = layer_norm(alpha * x + mix, ln1_g[l], ln1_b[l])
        x = layer_norm(alpha * h + swiglu_ffn(h, w_gate_up[l], w_down[l]), ln2_g[l], ln2_b[l])
    return x
```

```python
import numpy as np
from contextlib import ExitStack
import concourse.bass as bass
import concourse.mybir as mybir
from concourse.bass_utils import run_bass_kernel_spmd

F32 = mybir.dt.float32
BF16 = mybir.dt.bfloat16
AF = mybir.ActivationFunctionType
ALU = mybir.AluOpType

N_CORES = 8
D = 1024
SEQ = 2048
TOK = 4 * SEQ
NBLK = TOK // 128
NMT = NBLK // 4
DFF = 2816
NJ = DFF // 128
IN_COLS = 1536
ALPHA = float(2.0 ** 0.25)
EPS = 1e-5
WINDOWS = (2, 4, 8, 16)
H_RING = 8
DN_GROUPS = (4, 4, 4, 4, 4, 2)


class Sched:
    ENGS = ("pe", "act", "dve", "pool", "sp")

    def __init__(self, eng_sems, dma_sems):
        self.eng_sems = eng_sems
        self.dma_sems = dma_sems
        self.count = {e: 0 for e in self.ENGS}
        self.prog = {e: [] for e in self.ENGS}
        self.waited = {e: {} for e in self.ENGS}
        self.last_write = {}
        self.reads = {}
        self.dma_rr = {q: 0 for q in dma_sems}
        self.dma_cnt = {q: [0] * len(dma_sems[q]) for q in dma_sems}

    def _need(self, eng, tok, raw, strict):
        sem, val, prod, is_dma = tok
        if not strict and not is_dma and prod == eng and not raw:
            return None
        if self.waited[eng].get(id(sem), 0) >= val:
            return None
        return tok

    def _deps(self, eng, reads, writes, strict=False):
        toks = []
        for b in reads:
            t = self.last_write.get(b)
            if t is not None and self._need(eng, t, True, strict):
                toks.append(t)
        for b in writes:
            t = self.last_write.get(b)
            if t is not None and self._need(eng, t, False, strict):
                toks.append(t)
            for t in self.reads.get(b, ()):
                if self._need(eng, t, False, strict):
                    toks.append(t)
        best = {}
        for sem, val, prod, is_dma in toks:
            k = id(sem)
            if k not in best or best[k][1] < val:
                best[k] = (sem, val)
        return list(best.values())

    def _emit_waits(self, eng, waits):
        for sem, val in waits:
            self.waited[eng][id(sem)] = max(self.waited[eng].get(id(sem), 0), val)
            self.prog[eng].append(("wait", sem, val))

    def _commit(self, tok, reads, writes):
        for b in reads:
            self.reads.setdefault(b, []).append(tok)
        for b in writes:
            self.last_write[b] = tok
            self.reads[b] = []

    def task(self, eng, fn, reads=(), writes=(), excl=()):
        reads = list(reads) + list(excl)
        writes = list(writes) + list(excl)
        self._emit_waits(eng, self._deps(eng, reads, writes))
        self.count[eng] += 1
        tok = (self.eng_sems[eng], self.count[eng], eng, False)
        self.prog[eng].append(("op", fn, self.eng_sems[eng], 1))
        self._commit(tok, reads, writes)
        return tok

    def dma(self, q, out, in_, reads=(), writes=()):
        waits = self._deps(q, reads, writes, strict=True)
        i = self.dma_rr[q]
        self.dma_rr[q] = (i + 1) % len(self.dma_sems[q])
        sem = self.dma_sems[q][i]
        prev = self.dma_cnt[q][i]
        if prev > 0 and self.waited[q].get(id(sem), 0) < prev:
            waits.append((sem, prev))
        self._emit_waits(q, waits)
        self.dma_cnt[q][i] = prev + 16
        tok = (sem, prev + 16, None, True)

        def fn(e, out=out, in_=in_):
            return e.dma_start(out=out, in_=in_)
        self.prog[q].append(("op", fn, sem, 16))
        self._commit(tok, reads, writes)
        return tok

    def final_wait(self, eng, toks):
        best = {}
        for sem, val, prod, is_dma in toks:
            k = id(sem)
            if k not in best or best[k][1] < val:
                best[k] = (sem, val)
        self._emit_waits(eng, list(best.values()))

    def run(self, eng, handle):
        for item in self.prog[eng]:
            if item[0] == "wait":
                handle.wait_ge(item[1], item[2])
            else:
                item[1](handle).then_inc(item[2], item[3])


def build_nc():
    nc = bass.Bass("TRN2", target_bir_lowering=False)

    def din(name, shape):
        return nc.dram_tensor(name, list(shape), F32, kind="ExternalInput").ap()

    x_d = din("x", [TOK, D])
    w_in_d = din("w_in", [D, IN_COLS])
    pool_w_d = din("pool_w", [4, 128, 128])
    scaleB_d = din("scaleB", [128, 4, 128])
    sgB_d = din("sgu_gB", [128, 512])
    sbB_d = din("sgu_bB", [128, 512])
    sgu_w_d = din("sgu_w", [4, 128, 128])
    bsT_d = din("sgu_bT", [128, 4])
    w_out_d = din("w_out", [D, D])
    g1B_d = din("ln1_gB", [128, D])
    b1B_d = din("ln1_bB", [128, D])
    w_gu_d = din("w_gate_up", [D, 2 * DFF])
    w_dn_d = din("w_down", [DFF, D])
    g2B_d = din("ln2_gB", [128, D])
    b2B_d = din("ln2_bB", [128, D])
    ident_d = din("c_ident", [128, 128])
    tril_d = din("c_tril", [128, 4, 128])
    pfirst_d = din("c_pfirst", [128, 4, 128])
    prest_d = din("c_prest", [128, 4, 128])
    phalo_d = din("c_phalo", [128, 4, 128])
    icf_d = din("c_icfirst", [128, 4, 128])
    icr_d = din("c_icrest", [128, 4, 128])
    out_d = nc.dram_tensor("out", [TOK, D], F32, kind="ExternalOutput").ap()
    scr_gu = nc.dram_tensor("scr_gu", [NJ, 128, 8, 256], BF16, kind="Internal").ap()
    scr_dn = nc.dram_tensor("scr_dn", [2, 6, 128, 4, 512], BF16, kind="Internal").ap()

    with ExitStack() as es:
        def sb(name, shape, dt):
            return es.enter_context(nc.sbuf_tensor("sb_" + name, list(shape), dt))

        win = sb("win", [128, 8, IN_COLS], BF16)
        wout = sb("wout", [128, 8, D], BF16)
        g1B = sb("g1B", [128, D], F32)
        b1B = sb("b1B", [128, D], F32)
        g2B = sb("g2B", [128, D], F32)
        b2B = sb("b2B", [128, D], F32)
        sgB = sb("sgB", [128, 512], F32)
        sbB = sb("sbB", [128, 512], F32)
        scf = sb("scf", [128, 4, 128], F32)
        scr_ = sb("scr", [128, 4, 128], F32)
        pfirst = sb("pfirst", [128, 4, 128], BF16)
        prest = sb("prest", [128, 4, 128], BF16)
        phalo = sb("phalo", [128, 4, 128], BF16)
        poolw = sb("poolw", [128, 4, 128], BF16)
        wsT = sb("wsT", [128, 4, 128], BF16)
        ident = sb("ident", [128, 128], BF16)
        bsT = sb("bsT", [128, 4], F32)
        lnst = sb("lnst", [128, 8, 16], F32)
        XR = 3
        xr = sb("xr", [128, XR, D], F32)
        xbr = sb("xbr", [128, 1, D], BF16)
        xTr = sb("xTr", [128, 2, 8, 128], BF16)
        xpr = sb("xpr", [128, 3, 512], BF16)
        gur = sb("gur", [128, 2, 512], F32)
        gvr = sb("gvr", [128, 2, 512], F32)
        vnr = sb("vnr", [128, 2, 512], BF16)
        pTr = sb("pTr", [128, 2, 4, 128], BF16)
        sgr = sb("sgr", [128, 2, 512], BF16)
        mixr = sb("mixr", [128, 2, 8, 128], BF16)
        hr = sb("hr", [128, H_RING, D], F32)
        HB = 4
        hbr = sb("hbr", [128, HB, D], BF16)
        hT = sb("hT", [128, 8, 512], BF16)
        SIL = 2
        silr = sb("silr", [128, SIL, 512], F32)
        aT = sb("aT", [128, NJ, 512], BF16)
        WG = 3
        wgur = sb("wgur", [128, WG, 8, 256], BF16)
        WD = 3
        wdnr = sb("wdnr", [128, WD, 4, 512], BF16)
        ps = es.enter_context(nc.psum_tensor("ps", [128, 8, 512], F32))

        eng_sems = {e: es.enter_context(nc.semaphore("sem_" + e)) for e in Sched.ENGS}
        dma_sems = {q: [es.enter_context(nc.semaphore(f"dsem_{q}{i}")) for i in range(8)]
                    for q in ("sp", "pool")}
        S = Sched(eng_sems, dma_sems)

        from collections import deque
        free_banks = deque(range(8))

        def bank():
            b = free_banks.popleft()
            free_banks.append(b)
            return b

        def bank_hold():
            return free_banks.popleft()

        def bank_release(b):
            free_banks.append(b)

        def bn(b):
            return f"psb{b}"

        ln_ctr = [0]
        pending_sqrt = []

        def ln_a(src, ncols, bufname):
            r = ln_ctr[0] % 8
            ln_ctr[0] += 1
            nch = ncols // 512
            st, mv_, sd = f"lnst{r}", f"lnmv{r}", f"lnsd{r}"

            def f_stats(v):
                for c in range(nch):
                    ins = v.bn_stats(out=lnst[:, r, 6 * c:6 * c + 6], in_=src[:, c * 512:(c + 1) * 512])
                return ins
            S.task("dve", f_stats, reads=[bufname], writes=[st])
            S.task("dve", lambda v: v.bn_aggr(out=lnst[:, r, 12:14],
                                              in_=lnst[:, r, 0:6 * nch].rearrange("p (c s) -> p c s", s=6)),
                   reads=[st], writes=[mv_])
            pending_sqrt.append(r)
            return r

        def flush_sqrt():
            for r in pending_sqrt:
                S.task("act", lambda a, r=r: a.activation(out=lnst[:, r, 14:15], in_=lnst[:, r, 13:14],
                                                          func=AF.Sqrt, bias=EPS, scale=1.0),
                       reads=[f"lnmv{r}"], writes=[f"lnsd{r}"])
            pending_sqrt.clear()

        def ln_b(r, src, gB_, bB_, dst, bufname, dstname=None, gname=None, bname=None):
            mv_, sd, rs = f"lnmv{r}", f"lnsd{r}", f"lnrs{r}"
            if r in pending_sqrt:
                flush_sqrt()
            S.task("dve", lambda v: v.reciprocal(out=lnst[:, r, 15:16], in_=lnst[:, r, 14:15]),
                   reads=[sd], writes=[rs])
            S.task("dve", lambda v: v.scalar_tensor_tensor(out=src, in0=src, scalar=lnst[:, r, 12:13], in1=gB_,
                                                           op0=ALU.subtract, op1=ALU.mult),
                   reads=[bufname, mv_, gname], writes=[bufname])
            dn = dstname or bufname
            S.task("dve", lambda v: v.scalar_tensor_tensor(out=dst, in0=src, scalar=lnst[:, r, 15:16], in1=bB_,
                                                           op0=ALU.mult, op1=ALU.add),
                   reads=[bufname, rs, bname], writes=[dn])

        def tr_pe(src_ap_fn, n, src_names, c):
            b = bank()
            c["trb"] = b
            pv = ps[:, b, :].bitcast(BF16)

            def f_tr(t):
                for k in range(n):
                    ins = t.transpose(pv[:, k * 128:(k + 1) * 128], src_ap_fn(k), ident[:])
                return ins
            S.task("pe", f_tr, reads=list(src_names) + ["ident"], excl=[bn(b)])

        def tr_evac(n, dst_ap, dst_names, c, eng="dve"):
            b = c["trb"]
            src = ps[:, b, :].bitcast(BF16)[:, 0:n * 128].rearrange("p (k t) -> p k t", k=n)
            if eng == "dve":
                S.task("dve", lambda v: v.tensor_copy(out=dst_ap, in_=src), excl=[bn(b)], writes=dst_names)
            else:
                S.task("act", lambda a: a.activation(out=dst_ap, in_=src, func=AF.Copy),
                       excl=[bn(b)], writes=dst_names)

        def load_x(gb):
            if gb >= NBLK:
                return
            s = gb % XR
            S.dma("sp", xr[:, s, :], x_d[gb * 128:(gb + 1) * 128, :], writes=[f"x{s}"])

        load_x(0)
        S.dma("pool", ident[:], ident_d, writes=["ident"])
        S.dma("pool", win[:], w_in_d.rearrange("(k p) c -> p k c", p=128), writes=["win"])
        S.dma("sp", gur[:, 0, :].rearrange("p (h s) -> p h s", h=4), sgu_w_d.rearrange("h t s -> t h s"),
              writes=["gu0"])
        S.dma("sp", gvr[:, 0, :].rearrange("p (h s) -> p h s", h=4), tril_d, writes=["gv0"])
        S.dma("sp", sgB[:], sgB_d, writes=["sgB"])
        S.dma("sp", sbB[:], sbB_d, writes=["sbB"])
        S.dma("sp", bsT[:], bsT_d, writes=["bsT"])
        S.dma("sp", gur[:, 1, :].rearrange("p (g t) -> p g t", g=4), scaleB_d, writes=["gu1"])
        S.dma("sp", gvr[:, 1, :].rearrange("p (g t) -> p g t", g=4), icf_d, writes=["gv1"])
        for gb in range(1, XR):
            load_x(gb)
        S.dma("pool", pfirst[:], pfirst_d, writes=["pfirst"])
        S.dma("pool", prest[:], prest_d, writes=["prest"])
        S.dma("pool", phalo[:], phalo_d, writes=["phalo"])
        S.dma("pool", poolw[:], pool_w_d.rearrange("g c d -> c g d"), writes=["poolw"])
        S.dma("pool", wout[:], w_out_d.rearrange("(k p) c -> p k c", p=128), writes=["wout"])
        S.dma("sp", g1B[:], g1B_d, writes=["g1B"])
        S.dma("sp", b1B[:], b1B_d, writes=["b1B"])

        def late_prologue():
            S.task("dve", lambda v: v.tensor_tensor(out=vnr[:, 0, :], in0=gur[:, 0, :], in1=gvr[:, 0, :],
                                                    op=ALU.mult),
                   reads=["gu0", "gv0"], writes=["vn0"])
            c_ws = {}
            tr_pe(lambda k: vnr[:, 0, k * 128:(k + 1) * 128], 4, ["vn0"], c_ws)
            tr_evac(4, wsT[:], ["wsT"], c_ws)
            S.task("dve", lambda v: v.tensor_tensor(out=scf[:].rearrange("p g t -> p (g t)"), in0=gur[:, 1, :],
                                                    in1=gvr[:, 1, :], op=ALU.mult),
                   reads=["gu1", "gv1"], writes=["scf"])
            S.dma("sp", gvr[:, 1, :].rearrange("p (g t) -> p g t", g=4), icr_d, writes=["gv1"])
            S.task("dve", lambda v: v.tensor_tensor(out=scr_[:].rearrange("p g t -> p (g t)"), in0=gur[:, 1, :],
                                                    in1=gvr[:, 1, :], op=ALU.mult),
                   reads=["gu1", "gv1"], writes=["scr"])
            S.dma("sp", g2B[:], g2B_d, writes=["g2B"])
            S.dma("sp", b2B[:], b2B_d, writes=["b2B"])

        w_gu_v = w_gu_d.rearrange("(k p) f -> p k f", p=128)
        w_dn_v = w_dn_d.rearrange("(k p) c -> p k c", p=128)

        def conv_gu(j):
            S.dma("pool", scr_gu[j, :, :, 0:128], w_gu_v[:, :, j * 128:(j + 1) * 128], writes=[f"scrgu{j}a"])
            S.dma("pool", scr_gu[j, :, :, 128:256], w_gu_v[:, :, DFF + j * 128:DFF + (j + 1) * 128],
                  writes=[f"scrgu{j}b"])

        def conv_dn(half, kg):
            nk = DN_GROUPS[kg]
            k0 = 4 * kg
            S.dma("pool", scr_dn[half, kg, :, 0:nk, :], w_dn_v[:, k0:k0 + nk, half * 512:(half + 1) * 512],
                  writes=[f"scrdn{half}_{kg}"])

        for j in range(NJ):
            conv_gu(j)
        for half in range(2):
            for kg in range(len(DN_GROUPS)):
                conv_dn(half, kg)

        ctx = {}

        def P0(gb):
            xs = gb % XR

            def post():
                if True:
                    S.task("act", lambda a: a.activation(out=xbr[:, 0, :], in_=xr[:, xs, :], func=AF.Copy),
                           reads=[f"x{xs}"], writes=["xb0"])
                else:
                    S.task("pool", lambda g: g.tensor_copy(out=xbr[:, 0, :], in_=xr[:, xs, :]),
                           reads=[f"x{xs}"], writes=["xb0"])
            return None, post

        def P1(gb):
            s2 = gb % 2
            c = {}
            return (lambda: tr_pe(lambda k: xbr[:, 0, k * 128:(k + 1) * 128], 8, ["xb0"], c),
                    lambda: tr_evac(8, xTr[:, s2, :, :], [f"xT{s2}"], c))

        def P2(gb):
            s2, s3 = gb % 2, gb % 3
            c = ctx.setdefault(gb, {})

            def pe():
                bp, bu, bv = bank(), bank(), bank()
                c["in"] = (bp, bu, bv)

                def f_in(t):
                    for k in range(8):
                        for n_, b_ in enumerate((bp, bu, bv)):
                            ins = t.matmul(ps[:, b_, :], lhsT=xTr[:, s2, k, :],
                                           rhs=win[:, k, n_ * 512:(n_ + 1) * 512], start=(k == 0), stop=(k == 7))
                    return ins
                S.task("pe", f_in, reads=[f"xT{s2}", "win"], excl=[bn(bp), bn(bu), bn(bv)])

            def post():
                bp, bu, bv = c["in"]
                S.task("act", lambda a: a.activation(out=gvr[:, s2, :], in_=ps[:, bv, :], func=AF.Gelu),
                       excl=[bn(bv)], writes=[f"gv{s2}"])
                S.task("act", lambda a: a.activation(out=gur[:, s2, :], in_=ps[:, bu, :], func=AF.Gelu),
                       excl=[bn(bu)], writes=[f"gu{s2}"])
                S.task("act", lambda a: a.activation(out=xpr[:, s3, :], in_=ps[:, bp, :], func=AF.Copy),
                       excl=[bn(bp)], writes=[f"xp{s3}"])
                c["lnv"] = ln_a(gvr[:, s2, :], 512, f"gv{s2}")
            return pe, post

        def P2b(gb):
            s2 = gb % 2
            c = ctx.setdefault(gb, {})
            return None, (lambda: ln_b(c["lnv"], gvr[:, s2, :], sgB[:], sbB[:], vnr[:, s2, :], f"gv{s2}", f"vn{s2}", gname="sgB", bname="sbB"))

        def P3(gb):
            s2, s3, sp3 = gb % 2, gb % 3, (gb - 1) % 3
            first = (gb % 16 == 0)
            pc, pcn = (pfirst, "pfirst") if first else (prest, "prest")
            c = ctx.setdefault(gb, {})

            def pe():
                bq = bank()
                c["bq"] = bq

                def f_pool(t):
                    for g in range(4):
                        ins = t.matmul(ps[:, bq, g * 128:(g + 1) * 128], lhsT=xpr[:, s3, g * 128:(g + 1) * 128],
                                       rhs=pc[:, g, :], start=True, stop=first)
                        if not first:
                            ins = t.matmul(ps[:, bq, g * 128:(g + 1) * 128],
                                           lhsT=xpr[:, sp3, g * 128:(g + 1) * 128],
                                           rhs=phalo[:, g, :], start=False, stop=True)
                    return ins
                rd = [f"xp{s3}", pcn] + ([] if first else [f"xp{sp3}", "phalo"])
                S.task("pe", f_pool, reads=rd, excl=[bn(bq)])

            def post():
                bq = c["bq"]
                S.task("act", lambda a: a.activation(out=pTr[:, s2, :, :].rearrange("p g t -> p (g t)"),
                                                     in_=ps[:, bq, :], func=AF.Copy),
                       excl=[bn(bq)], writes=[f"pT{s2}"])
            return pe, post

        def P4p(gb):
            s2 = gb % 2
            first = (gb % 16 == 0)
            sc, scn = (scf, "scf") if first else (scr_, "scr")
            c = ctx.setdefault(gb, {})

            def pe():
                br = bank()
                c["br"] = br

                def f_pw(t):
                    for g in range(4):
                        ins = t.matmul(ps[:, br, g * 128:(g + 1) * 128], lhsT=poolw[:, g, :], rhs=pTr[:, s2, g, :],
                                       start=True, stop=True)
                    return ins
                S.task("pe", f_pw, reads=[f"pT{s2}", "poolw"], excl=[bn(br)])

            def post():
                br = c["br"]
                S.task("dve", lambda v: v.tensor_tensor(out=mixr[:, s2, 0:4, :],
                                                        in0=ps[:, br, :].rearrange("p (g t) -> p g t", g=4),
                                                        in1=sc[:], op=ALU.mult),
                       reads=[scn], excl=[bn(br)], writes=[f"mixA{s2}"])
            return pe, post

        def P4s(gb):
            s2 = gb % 2
            c = ctx.setdefault(gb, {})

            def pe():
                bs_ = bank()
                c["bs"] = bs_

                def f_sgu(t):
                    for h in range(4):
                        ins = t.matmul(ps[:, bs_, h * 128:(h + 1) * 128], lhsT=wsT[:, h, :],
                                       rhs=vnr[:, s2, h * 128:(h + 1) * 128], start=True, stop=True)
                    return ins
                S.task("pe", f_sgu, reads=[f"vn{s2}", "wsT"], excl=[bn(bs_)])

            def post():
                bs_ = c["bs"]

                def f_gate(v):
                    for h in range(4):
                        ins = v.scalar_tensor_tensor(out=sgr[:, s2, h * 128:(h + 1) * 128],
                                                     in0=ps[:, bs_, h * 128:(h + 1) * 128], scalar=bsT[:, h:h + 1],
                                                     in1=gur[:, s2, h * 128:(h + 1) * 128],
                                                     op0=ALU.add, op1=ALU.mult)
                    return ins
                S.task("dve", f_gate, reads=[f"gu{s2}", "bsT"], excl=[bn(bs_)], writes=[f"sg{s2}"])
            return pe, post

        def P5(gb):
            s2 = gb % 2
            c = {}
            return (lambda: tr_pe(lambda k: sgr[:, s2, k * 128:(k + 1) * 128], 4, [f"sg{s2}"], c),
                    lambda: tr_evac(4, mixr[:, s2, 4:8, :], [f"mixB{s2}"], c, eng="act"))

        def P6(gb):
            s2, xs, hs = gb % 2, gb % XR, gb % H_RING
            c = ctx.setdefault(gb, {})

            def pe():
                b0, b1 = bank(), bank()
                c["out"] = (b0, b1)

                def f_out(t):
                    for k in range(8):
                        for n_, b_ in enumerate((b0, b1)):
                            ins = t.matmul(ps[:, b_, :], lhsT=mixr[:, s2, k, :],
                                           rhs=wout[:, k, n_ * 512:(n_ + 1) * 512], start=(k == 0), stop=(k == 7))
                    return ins
                S.task("pe", f_out, reads=[f"mixA{s2}", f"mixB{s2}", "wout"], excl=[bn(b0), bn(b1)])

            def post():
                for n_, b_ in enumerate(c["out"]):
                    S.task("dve", lambda v, n_=n_, b_=b_: v.scalar_tensor_tensor(
                        out=hr[:, hs, n_ * 512:(n_ + 1) * 512], in0=xr[:, xs, n_ * 512:(n_ + 1) * 512],
                        scalar=ALPHA, in1=ps[:, b_, :], op0=ALU.mult, op1=ALU.add),
                        reads=[f"x{xs}"], excl=[bn(b_)], writes=[f"h{hs}"])
                load_x(gb + XR)
                c["ln1"] = ln_a(hr[:, hs, :], D, f"h{hs}")
            return pe, post

        def P6b(gb):
            hs = gb % H_RING
            c = ctx.setdefault(gb, {})
            return None, (lambda: ln_b(c["ln1"], hr[:, hs, :], g1B[:], b1B[:], hr[:, hs, :], f"h{hs}", gname="g1B", bname="b1B"))

        def P6c(gb):
            hs, hb = gb % H_RING, gb % HB

            def post():
                if True:
                    S.task("act", lambda a: a.activation(out=hbr[:, hb, :], in_=hr[:, hs, :], func=AF.Copy),
                           reads=[f"h{hs}"], writes=[f"hb{hb}"])
                else:
                    S.task("pool", lambda g: g.tensor_copy(out=hbr[:, hb, :], in_=hr[:, hs, :]),
                           reads=[f"h{hs}"], writes=[f"hb{hb}"])
            return None, post

        def P7(gb):
            hb, pos = gb % HB, gb % 4
            c = {}
            return (lambda: tr_pe(lambda k: hbr[:, hb, k * 128:(k + 1) * 128], 8, [f"hb{hb}"], c),
                    lambda: tr_evac(8, hT[:, :, pos * 128:(pos + 1) * 128], [f"hT{pos}"], c))

        PSTEPS = (P0, P1, P2, P2b, P3, P4p, P4s, P5, P6, P6b, P6c)
        PBANKS = {P0: 0, P1: 1, P2: 3, P2b: 0, P3: 1, P4p: 1, P4s: 1, P5: 1, P6: 2, P6b: 0, P6c: 0, P7: 1}
        PCOST = {P0: 0.3, P1: 0.9, P2: 1.2, P2b: 1.8, P3: 0.3, P4p: 0.9, P4s: 1.3, P5: 0.3, P6: 3.6, P6b: 2.5,
                 P6c: 0.3, P7: 0.9}
        SKEW = 5

        def mk(fn, gb):
            pe, post = fn(gb)
            return (pe, post, PBANKS[fn], gb, False, PCOST[fn])

        def mixer_steps(g0):
            keyed = sorted([(max(0, k + SKEW * b - (3 if k == 0 else 0)), b, k)
                            for b in range(4) for k in range(len(PSTEPS))])
            return [mk(PSTEPS[k], g0 + b) for _, b, k in keyed]

        gu_issue = [0]
        dn_issue = [0]

        def issue_wgu(j):
            s = gu_issue[0] % WG
            gu_issue[0] += 1
            S.dma("sp", wgur[:, s, :, :], scr_gu[j], reads=[f"scrgu{j}a", f"scrgu{j}b"], writes=[f"wgu{s}"])
            return s

        def issue_wdn(half, kg):
            s = dn_issue[0] % WD
            dn_issue[0] += 1
            nk = DN_GROUPS[kg]
            S.dma("sp", wdnr[:, s, 0:nk, :], scr_dn[half, kg, :, 0:nk, :], reads=[f"scrdn{half}_{kg}"],
                  writes=[f"wdn{s}"])
            return s

        sil_ctr = [0]
        out_toks = []

        def gate_up_chunk(j, s):
            c = {}

            def pe():
                bg, bu = bank(), bank()
                c["b"] = (bg, bu)

                def f_gu(t):
                    for k in range(8):
                        ins = t.matmul(ps[:, bg, :], lhsT=wgur[:, s, k, 0:128], rhs=hT[:, k, :],
                                       start=(k == 0), stop=(k == 7))
                    for k in range(8):
                        ins = t.matmul(ps[:, bu, :], lhsT=wgur[:, s, k, 128:256], rhs=hT[:, k, :],
                                       start=(k == 0), stop=(k == 7))
                    return ins
                S.task("pe", f_gu, reads=[f"wgu{s}", "hT0", "hT1", "hT2", "hT3"], excl=[bn(bg), bn(bu)])

            def post():
                bg, bu = c["b"]
                q = sil_ctr[0] % SIL
                sil_ctr[0] += 1
                S.task("act", lambda a: a.activation(out=silr[:, q, :], in_=ps[:, bg, :], func=AF.Tanh, scale=0.5),
                       excl=[bn(bg)], writes=[f"sil{q}"])
                S.task("dve", lambda v: v.scalar_tensor_tensor(out=silr[:, q, :], in0=silr[:, q, :], scalar=1.0,
                                                               in1=ps[:, bg, :], op0=ALU.add, op1=ALU.mult),
                       reads=[f"sil{q}"], excl=[bn(bg)], writes=[f"sil{q}"])
                S.task("dve", lambda v: v.scalar_tensor_tensor(out=aT[:, j, :], in0=silr[:, q, :], scalar=0.5,
                                                               in1=ps[:, bu, :], op0=ALU.mult, op1=ALU.mult),
                       reads=[f"sil{q}"], excl=[bn(bu)], writes=[f"aT{j}"])
            return pe, post, 2, None, False, 0.0

        def tail_steps(mt):
            steps = []
            stores = []
            slot = {}
            for b in range(4):
                gb = mt * 4 + b
                hs = gb % H_RING

                def a(gb=gb, hs=hs):
                    slot[gb] = ln_a(hr[:, hs, :], D, f"h{hs}")

                def b_(gb=gb, hs=hs):
                    ln_b(slot[gb], hr[:, hs, :], g2B[:], b2B[:], hr[:, hs, :], f"h{hs}", gname="g2B", bname="b2B")

                def c_(gb=gb, hs=hs):
                    out_toks.append(S.dma("sp", out_d[gb * 128:(gb + 1) * 128, :], hr[:, hs, :],
                                          reads=[f"h{hs}"]))
                steps.append([(None, a, 0, None, False, 1.8), (None, b_, 0, None, False, 2.5)])
                stores.append((None, c_, 0, None, False, 0.3))
            keyed = sorted([(k + b, b, k) for b in range(4) for k in range(2)])
            return [steps[b][k] for _, b, k in keyed], stores

        def run_slot(parts):
            flush_sqrt()
            pending = []
            used = 0

            def flush():
                nonlocal used
                for p in pending:
                    p()
                pending.clear()
                used = 0
            tags = set()
            pst_busy = False
            for pe, post, nb, tag, uses_pst, _cost in parts:
                if pe is not None:
                    if used + nb > len(free_banks) or (tag is not None and tag in tags) or (uses_pst and pst_busy):
                        flush()
                        tags.clear()
                        pst_busy = False
                    pe()
                    used += nb
                    pst_busy = pst_busy or uses_pst
                if post is not None:
                    pending.append(post)
                    if tag is not None:
                        tags.add(tag)
            flush()

        pre_gu = {}

        def ffn(mt, fill, n_fill_slots):
            PF = WG - 1
            NG = len(DN_GROUPS)
            PFD = WD - 1
            chunks = [(half, kg) for half in range(2) for kg in range(NG)]
            fi = 0

            cap = [1.0] * NJ + [1.7] * len(chunks)
            cap = [c_ if i < n_fill_slots else 0.0 for i, c_ in enumerate(cap)]
            cap_tot = sum(cap)
            cost_tot = sum(f_[5] for f_ in fill)
            cap_cum = [0.0]
            taken_cost = [0.0]

            def take(slot_idx):
                nonlocal fi
                cap_cum[0] += cap[slot_idx]
                tgt_cost = cost_tot * cap_cum[0] / cap_tot
                out = []
                last = slot_idx >= n_fill_slots - 1
                while fi < len(fill) and (last or taken_cost[0] + 0.5 * fill[fi][5] <= tgt_cost):
                    taken_cost[0] += fill[fi][5]
                    out.append(fill[fi])
                    fi += 1
                return out

            slots = dict(pre_gu.pop(mt, {}))
            for j in range(PF):
                if j not in slots:
                    slots[j] = issue_wgu(j)
            dslots = {}
            for j in range(NJ):
                if j + PF < NJ:
                    slots[j + PF] = issue_wgu(j + PF)
                if j == NJ - 2:
                    for c in range(PFD):
                        dslots[c] = issue_wdn(*chunks[c])
                run_slot(take(j) + [gate_up_chunk(j, slots[j])])
            bd = None
            for c, (half, kg) in enumerate(chunks):
                if c + PFD < len(chunks):
                    dslots[c + PFD] = issue_wdn(*chunks[c + PFD])
                if c == len(chunks) - 3 and mt + 1 < NMT:
                    pre_gu[mt + 1] = {j: issue_wgu(j) for j in range(PF)}
                if kg == 0:
                    bd = [bank_hold() for _ in range(4)]
                s = dslots[c]
                nk = DN_GROUPS[kg]

                def pe(s=s, kg=kg, bd=bd, nk=nk):
                    def f_dn(t):
                        for kk in range(nk):
                            k = 4 * kg + kk
                            for b in range(4):
                                ins = t.matmul(ps[:, bd[b], :], lhsT=aT[:, k, b * 128:(b + 1) * 128],
                                               rhs=wdnr[:, s, kk, :], start=(k == 0), stop=(k == NJ - 1))
                        return ins
                    S.task("pe", f_dn, reads=[f"wdn{s}"] + [f"aT{4 * kg + kk}" for kk in range(nk)],
                           excl=[bn(b_) for b_ in bd])

                post = None
                if kg == NG - 1:
                    def post(bd=bd, half=half):
                        for b in range(4):
                            gb = mt * 4 + b
                            hs = gb % H_RING
                            S.task("dve", lambda v, hs=hs, b_=bd[b]: v.scalar_tensor_tensor(
                                out=hr[:, hs, half * 512:(half + 1) * 512],
                                in0=hr[:, hs, half * 512:(half + 1) * 512],
                                scalar=ALPHA, in1=ps[:, b_, :], op0=ALU.mult, op1=ALU.add),
                                reads=[f"h{hs}"], excl=[bn(bd[b])], writes=[f"h{hs}"])
                            bank_release(bd[b])
                run_slot(take(NJ + c) + [(pe, post, 0, None, False, 0.0)])
            assert fi == len(fill), (fi, len(fill))

        for i, st_ in enumerate(mixer_steps(0)):
            if i == 0:
                late_prologue()
            run_slot([st_])
        for b in range(4):
            run_slot([mk(P7, b)])
        tail, stores = [], []
        NSLOTS = NJ + 2 * len(DN_GROUPS)
        for mt in range(NMT):
            fill = []
            if mt + 1 < NMT:
                g1 = (mt + 1) * 4
                ms = mixer_steps(g1)
                fill += ms[:2]
                ms = ms[2:]
            fill += tail
            if mt + 1 < NMT:
                for i, st_ in enumerate(stores):
                    ms.insert(6 + 4 * i, st_)
                fill += ms
                fill += [mk(P7, g1 + b) for b in range(4)]
            else:
                fill += stores
            ffn(mt, fill, NSLOTS - 3)
            tail, stores = tail_steps(mt)
        for st_ in tail + stores:
            run_slot([st_])
        S.final_wait("sp", out_toks)

        with nc.Block() as block:
            @block.sync
            def _(e):
                S.run("sp", e)

            @block.scalar
            def _(e):
                S.run("act", e)

            @block.vector
            def _(e):
                S.run("dve", e)

            @block.gpsimd
            def _(e):
                S.run("pool", e)

            @block.tensor
            def _(e):
                S.run("pe", e)
    return nc


def _constants():
    c = {}
    c["c_ident"] = np.eye(128, dtype=np.float32)
    s = np.arange(128)[:, None]
    t = np.arange(128)[None, :]
    tril_ts = (np.arange(128)[None, :] <= np.arange(128)[:, None]).astype(np.float32)
    c["c_tril"] = np.ascontiguousarray(np.broadcast_to(tril_ts[:, None, :], (128, 4, 128))).astype(np.float32)
    pfirst = np.zeros((128, 4, 128), np.float32)
    prest = np.zeros((128, 4, 128), np.float32)
    phalo = np.zeros((128, 4, 128), np.float32)
    icf = np.zeros((128, 4, 128), np.float32)
    icr = np.zeros((128, 4, 128), np.float32)
    for g, w in enumerate(WINDOWS):
        band = ((s <= t) & (s > t - w)).astype(np.float32)
        cnt_first = np.minimum(np.arange(128) + 1, w).astype(np.float32)
        eye = (s == t).astype(np.float32)
        pfirst[:, g, :] = band - eye * cnt_first[None, :]
        prest[:, g, :] = band - eye * float(w)
        phalo[:, g, :] = (s >= 129 + t - w).astype(np.float32)
        icf[:, g, :] = (1.0 / cnt_first)[None, :]
        icr[:, g, :] = 1.0 / w
    c["c_pfirst"], c["c_prest"], c["c_phalo"] = pfirst, prest, phalo
    c["c_icfirst"], c["c_icrest"] = icf, icr
    return c


def _bcast(v, n=128):
    v = np.asarray(v, dtype=np.float32).reshape(1, -1)
    return np.ascontiguousarray(np.broadcast_to(v, (n, v.shape[1])))


_NC_CACHE = {}


def kernel(x, w_in, pool_w, pool_scale, sgu_ln_g, sgu_ln_b, sgu_w, sgu_b,
           w_out, ln1_g, ln1_b, w_gate_up, w_down, ln2_g, ln2_b):
    f = lambda a: np.ascontiguousarray(np.asarray(a, dtype=np.float32))
    x = f(x)
    B, Sq, Dm = x.shape
    assert (B, Sq, Dm) == (32, SEQ, D)
    shared = {
        "w_in": f(w_in)[0], "pool_w": f(pool_w)[0],
        "scaleB": np.ascontiguousarray(np.broadcast_to(
            f(pool_scale)[0].reshape(4, 128).T[:, :, None], (128, 4, 128))),
        "sgu_gB": _bcast(f(sgu_ln_g)[0]), "sgu_bB": _bcast(f(sgu_ln_b)[0]),
        "sgu_w": f(sgu_w)[0], "sgu_bT": np.ascontiguousarray(f(sgu_b)[0].T),
        "w_out": f(w_out)[0], "ln1_gB": _bcast(f(ln1_g)[0]), "ln1_bB": _bcast(f(ln1_b)[0]),
        "w_gate_up": f(w_gate_up)[0], "w_down": f(w_down)[0],
        "ln2_gB": _bcast(f(ln2_g)[0]), "ln2_bB": _bcast(f(ln2_b)[0]),
    }
    shared.update(_constants())
    xs = x.reshape(N_CORES, TOK, D)
    in_maps = [dict(shared, x=np.ascontiguousarray(xs[c])) for c in range(N_CORES)]
    if "nc" not in _NC_CACHE:
        _NC_CACHE["nc"] = build_nc()
    res = run_bass_kernel_spmd(_NC_CACHE["nc"], in_maps, core_ids=list(range(N_CORES)))
    out = np.stack([np.asarray(r["out"], dtype=np.float32) for r in res.results], axis=0)
    return out.reshape(B, Sq, Dm)
```

```python
import numpy as np
from contextlib import ExitStack
import concourse.bass as bass
import concourse.mybir as mybir
from concourse.bass_utils import run_bass_kernel_spmd

F32 = mybir.dt.float32
BF16 = mybir.dt.bfloat16
AF = mybir.ActivationFunctionType
ALU = mybir.AluOpType

N_CORES = 8
D = 1024
SEQ = 2048
TOK = 4 * SEQ
NBLK = TOK // 128
NMT = NBLK // 4
DFF = 2816
NJ = DFF // 128
IN_COLS = 1536
ALPHA = float(2.0 ** 0.25)
EPS = 1e-5
WINDOWS = (2, 4, 8, 16)
H_RING = 8
DN_GROUPS = (4, 4, 4, 4, 4, 2)


class Sched:
    ENGS = ("pe", "act", "dve", "pool", "sp")

    def __init__(self, eng_sems, dma_sems):
        self.eng_sems = eng_sems
        self.dma_sems = dma_sems
        self.count = {e: 0 for e in self.ENGS}
        self.prog = {e: [] for e in self.ENGS}
        self.waited = {e: {} for e in self.ENGS}
        self.last_write = {}
        self.reads = {}
        self.dma_rr = {q: 0 for q in dma_sems}
        self.dma_cnt = {q: [0] * len(dma_sems[q]) for q in dma_sems}

    def _need(self, eng, tok, raw, strict):
        sem, val, prod, is_dma = tok
        if not strict and not is_dma and prod == eng and not raw:
            return None
        if self.waited[eng].get(id(sem), 0) >= val:
            return None
        return tok

    def _deps(self, eng, reads, writes, strict=False):
        toks = []
        for b in reads:
            t = self.last_write.get(b)
            if t is not None and self._need(eng, t, True, strict):
                toks.append(t)
        for b in writes:
            t = self.last_write.get(b)
            if t is not None and self._need(eng, t, False, strict):
                toks.append(t)
            for t in self.reads.get(b, ()):
                if self._need(eng, t, False, strict):
                    toks.append(t)
        best = {}
        for sem, val, prod, is_dma in toks:
            k = id(sem)
            if k not in best or best[k][1] < val:
                best[k] = (sem, val)
        return list(best.values())

    def _emit_waits(self, eng, waits):
        for sem, val in waits:
            self.waited[eng][id(sem)] = max(self.waited[eng].get(id(sem), 0), val)
            self.prog[eng].append(("wait", sem, val))

    def _commit(self, tok, reads, writes):
        for b in reads:
            self.reads.setdefault(b, []).append(tok)
        for b in writes:
            self.last_write[b] = tok
            self.reads[b] = []

    def task(self, eng, fn, reads=(), writes=(), excl=()):
        reads = list(reads) + list(excl)
        writes = list(writes) + list(excl)
        self._emit_waits(eng, self._deps(eng, reads, writes))
        self.count[eng] += 1
        tok = (self.eng_sems[eng], self.count[eng], eng, False)
        self.prog[eng].append(("op", fn, self.eng_sems[eng], 1))
        self._commit(tok, reads, writes)
        return tok

    def dma(self, q, out, in_, reads=(), writes=()):
        waits = self._deps(q, reads, writes, strict=True)
        i = self.dma_rr[q]
        self.dma_rr[q] = (i + 1) % len(self.dma_sems[q])
        sem = self.dma_sems[q][i]
        prev = self.dma_cnt[q][i]
        if prev > 0 and self.waited[q].get(id(sem), 0) < prev:
            waits.append((sem, prev))
        self._emit_waits(q, waits)
        self.dma_cnt[q][i] = prev + 16
        tok = (sem, prev + 16, None, True)

        def fn(e, out=out, in_=in_):
            return e.dma_start(out=out, in_=in_)
        self.prog[q].append(("op", fn, sem, 16))
        self._commit(tok, reads, writes)
        return tok

    def final_wait(self, eng, toks):
        best = {}
        for sem, val, prod, is_dma in toks:
            k = id(sem)
            if k not in best or best[k][1] < val:
                best[k] = (sem, val)
        self._emit_waits(eng, list(best.values()))

    def run(self, eng, handle):
        for item in self.prog[eng]:
            if item[0] == "wait":
                handle.wait_ge(item[1], item[2])
            else:
                item[1](handle).then_inc(item[2], item[3])


def build_nc():
    nc = bass.Bass("TRN2", target_bir_lowering=False)

    def din(name, shape):
        return nc.dram_tensor(name, list(shape), F32, kind="ExternalInput").ap()

    x_d = din("x", [TOK, D])
    w_in_d = din("w_in", [D, IN_COLS])
    pool_w_d = din("pool_w", [4, 128, 128])
    scaleB_d = din("scaleB", [128, 4, 128])
    sgB_d = din("sgu_gB", [128, 512])
    sbB_d = din("sgu_bB", [128, 512])
    sgu_w_d = din("sgu_w", [4, 128, 128])
    bsT_d = din("sgu_bT", [128, 4])
    w_out_d = din("w_out", [D, D])
    g1B_d = din("ln1_gB", [128, D])
    b1B_d = din("ln1_bB", [128, D])
    w_gu_d = din("w_gate_up", [D, 2 * DFF])
    w_dn_d = din("w_down", [DFF, D])
    g2B_d = din("ln2_gB", [128, D])
    b2B_d = din("ln2_bB", [128, D])
    ident_d = din("c_ident", [128, 128])
    tril_d = din("c_tril", [128, 4, 128])
    pfirst_d = din("c_pfirst", [128, 4, 128])
    prest_d = din("c_prest", [128, 4, 128])
    phalo_d = din("c_phalo", [128, 4, 128])
    icf_d = din("c_icfirst", [128, 4, 128])
    icr_d = din("c_icrest", [128, 4, 128])
    out_d = nc.dram_tensor("out", [TOK, D], F32, kind="ExternalOutput").ap()
    scr_gu = nc.dram_tensor("scr_gu", [NJ, 128, 8, 256], BF16, kind="Internal").ap()
    scr_dn = nc.dram_tensor("scr_dn", [2, 6, 128, 4, 512], BF16, kind="Internal").ap()

    with ExitStack() as es:
        def sb(name, shape, dt):
            return es.enter_context(nc.sbuf_tensor("sb_" + name, list(shape), dt))

        win = sb("win", [128, 8, IN_COLS], BF16)
        wout = sb("wout", [128, 8, D], BF16)
        g1B = sb("g1B", [128, D], F32)
        b1B = sb("b1B", [128, D], F32)
        g2B = sb("g2B", [128, D], F32)
        b2B = sb("b2B", [128, D], F32)
        sgB = sb("sgB", [128, 512], F32)
        sbB = sb("sbB", [128, 512], F32)
        scf = sb("scf", [128, 4, 128], F32)
        scr_ = sb("scr", [128, 4, 128], F32)
        pfirst = sb("pfirst", [128, 4, 128], BF16)
        prest = sb("prest", [128, 4, 128], BF16)
        phalo = sb("phalo", [128, 4, 128], BF16)
        poolw = sb("poolw", [128, 4, 128], BF16)
        wsT = sb("wsT", [128, 4, 128], BF16)
        ident = sb("ident", [128, 128], BF16)
        bsT = sb("bsT", [128, 4], F32)
        lnst = sb("lnst", [128, 8, 16], F32)
        XR = 3
        xr = sb("xr", [128, XR, D], F32)
        xbr = sb("xbr", [128, 1, D], BF16)
        xTr = sb("xTr", [128, 2, 8, 128], BF16)
        xpr = sb("xpr", [128, 3, 512], BF16)
        gur = sb("gur", [128, 2, 512], F32)
        gvr = sb("gvr", [128, 2, 512], F32)
        vnr = sb("vnr", [128, 2, 512], BF16)
        pTr = sb("pTr", [128, 2, 4, 128], BF16)
        sgr = sb("sgr", [128, 2, 512], BF16)
        mixr = sb("mixr", [128, 2, 8, 128], BF16)
        hr = sb("hr", [128, H_RING, D], F32)
        HB = 4
        hbr = sb("hbr", [128, HB, D], BF16)
        hT = sb("hT", [128, 8, 512], BF16)
        SIL = 2
        silr = sb("silr", [128, SIL, 512], F32)
        aT = sb("aT", [128, NJ, 512], BF16)
        WG = 3
        wgur = sb("wgur", [128, WG, 8, 256], BF16)
        WD = 3
        wdnr = sb("wdnr", [128, WD, 4, 512], BF16)
        ps = es.enter_context(nc.psum_tensor("ps", [128, 8, 512], F32))

        eng_sems = {e: es.enter_context(nc.semaphore("sem_" + e)) for e in Sched.ENGS}
        dma_sems = {q: [es.enter_context(nc.semaphore(f"dsem_{q}{i}")) for i in range(8)]
                    for q in ("sp", "pool")}
        S = Sched(eng_sems, dma_sems)

        from collections import deque
        free_banks = deque(range(8))

        def bank():
            b = free_banks.popleft()
            free_banks.append(b)
            return b

        def bank_hold():
            return free_banks.popleft()

        def bank_release(b):
            free_banks.append(b)

        def bn(b):
            return f"psb{b}"

        ln_ctr = [0]
        pending_sqrt = []

        def ln_a(src, ncols, bufname):
            r = ln_ctr[0] % 8
            ln_ctr[0] += 1
            nch = ncols // 512
            st, mv_, sd = f"lnst{r}", f"lnmv{r}", f"lnsd{r}"

            def f_stats(v):
                for c in range(nch):
                    ins = v.bn_stats(out=lnst[:, r, 6 * c:6 * c + 6], in_=src[:, c * 512:(c + 1) * 512])
                return ins
            S.task("dve", f_stats, reads=[bufname], writes=[st])
            S.task("dve", lambda v: v.bn_aggr(out=lnst[:, r, 12:14],
                                              in_=lnst[:, r, 0:6 * nch].rearrange("p (c s) -> p c s", s=6)),
                   reads=[st], writes=[mv_])
            pending_sqrt.append(r)
            return r

        def flush_sqrt():
            for r in pending_sqrt:
                S.task("act", lambda a, r=r: a.activation(out=lnst[:, r, 14:15], in_=lnst[:, r, 13:14],
                                                          func=AF.Sqrt, bias=EPS, scale=1.0),
                       reads=[f"lnmv{r}"], writes=[f"lnsd{r}"])
            pending_sqrt.clear()

        def ln_b(r, src, gB_, bB_, dst, bufname, dstname=None, gname=None, bname=None):
            mv_, sd, rs = f"lnmv{r}", f"lnsd{r}", f"lnrs{r}"
            if r in pending_sqrt:
                flush_sqrt()
            S.task("dve", lambda v: v.reciprocal(out=lnst[:, r, 15:16], in_=lnst[:, r, 14:15]),
                   reads=[sd], writes=[rs])
            S.task("dve", lambda v: v.scalar_tensor_tensor(out=src, in0=src, scalar=lnst[:, r, 12:13], in1=gB_,
                                                           op0=ALU.subtract, op1=ALU.mult),
                   reads=[bufname, mv_, gname], writes=[bufname])
            dn = dstname or bufname
            S.task("dve", lambda v: v.scalar_tensor_tensor(out=dst, in0=src, scalar=lnst[:, r, 15:16], in1=bB_,
                                                           op0=ALU.mult, op1=ALU.add),
                   reads=[bufname, rs, bname], writes=[dn])

        def tr_pe(src_ap_fn, n, src_names, c):
            b = bank()
            c["trb"] = b
            pv = ps[:, b, :].bitcast(BF16)

            def f_tr(t):
                for k in range(n):
                    ins = t.transpose(pv[:, k * 128:(k + 1) * 128], src_ap_fn(k), ident[:])
                return ins
            S.task("pe", f_tr, reads=list(src_names) + ["ident"], excl=[bn(b)])

        def tr_evac(n, dst_ap, dst_names, c, eng="dve"):
            b = c["trb"]
            src = ps[:, b, :].bitcast(BF16)[:, 0:n * 128].rearrange("p (k t) -> p k t", k=n)
            if eng == "dve":
                S.task("dve", lambda v: v.tensor_copy(out=dst_ap, in_=src), excl=[bn(b)], writes=dst_names)
            else:
                S.task("act", lambda a: a.activation(out=dst_ap, in_=src, func=AF.Copy),
                       excl=[bn(b)], writes=dst_names)

        def load_x(gb):
            if gb >= NBLK:
                return
            s = gb % XR
            S.dma("sp", xr[:, s, :], x_d[gb * 128:(gb + 1) * 128, :], writes=[f"x{s}"])

        load_x(0)
        S.dma("pool", ident[:], ident_d, writes=["ident"])
        S.dma("pool", win[:], w_in_d.rearrange("(k p) c -> p k c", p=128), writes=["win"])
        S.dma("sp", gur[:, 0, :].rearrange("p (h s) -> p h s", h=4), sgu_w_d.rearrange("h t s -> t h s"),
              writes=["gu0"])
        S.dma("sp", gvr[:, 0, :].rearrange("p (h s) -> p h s", h=4), tril_d, writes=["gv0"])
        S.dma("sp", sgB[:], sgB_d, writes=["sgB"])
        S.dma("sp", sbB[:], sbB_d, writes=["sbB"])
        S.dma("sp", bsT[:], bsT_d, writes=["bsT"])
        S.dma("sp", gur[:, 1, :].rearrange("p (g t) -> p g t", g=4), scaleB_d, writes=["gu1"])
        S.dma("sp", gvr[:, 1, :].rearrange("p (g t) -> p g t", g=4), icf_d, writes=["gv1"])
        for gb in range(1, XR):
            load_x(gb)
        S.dma("pool", pfirst[:], pfirst_d, writes=["pfirst"])
        S.dma("pool", prest[:], prest_d, writes=["prest"])
        S.dma("pool", phalo[:], phalo_d, writes=["phalo"])
        S.dma("pool", poolw[:], pool_w_d.rearrange("g c d -> c g d"), writes=["poolw"])
        S.dma("pool", wout[:], w_out_d.rearrange("(k p) c -> p k c", p=128), writes=["wout"])
        S.dma("sp", g1B[:], g1B_d, writes=["g1B"])
        S.dma("sp", b1B[:], b1B_d, writes=["b1B"])

        def late_prologue():
            S.task("dve", lambda v: v.tensor_tensor(out=vnr[:, 0, :], in0=gur[:, 0, :], in1=gvr[:, 0, :],
                                                    op=ALU.mult),
                   reads=["gu0", "gv0"], writes=["vn0"])
            c_ws = {}
            tr_pe(lambda k: vnr[:, 0, k * 128:(k + 1) * 128], 4, ["vn0"], c_ws)
            tr_evac(4, wsT[:], ["wsT"], c_ws)
            S.task("dve", lambda v: v.tensor_tensor(out=scf[:].rearrange("p g t -> p (g t)"), in0=gur[:, 1, :],
                                                    in1=gvr[:, 1, :], op=ALU.mult),
                   reads=["gu1", "gv1"], writes=["scf"])
            S.dma("sp", gvr[:, 1, :].rearrange("p (g t) -> p g t", g=4), icr_d, writes=["gv1"])
            S.task("dve", lambda v: v.tensor_tensor(out=scr_[:].rearrange("p g t -> p (g t)"), in0=gur[:, 1, :],
                                                    in1=gvr[:, 1, :], op=ALU.mult),
                   reads=["gu1", "gv1"], writes=["scr"])
            S.dma("sp", g2B[:], g2B_d, writes=["g2B"])
            S.dma("sp", b2B[:], b2B_d, writes=["b2B"])

        w_gu_v = w_gu_d.rearrange("(k p) f -> p k f", p=128)
        w_dn_v = w_dn_d.rearrange("(k p) c -> p k c", p=128)

        def conv_gu(j):
            S.dma("pool", scr_gu[j, :, :, 0:128], w_gu_v[:, :, j * 128:(j + 1) * 128], writes=[f"scrgu{j}a"])
            S.dma("pool", scr_gu[j, :, :, 128:256], w_gu_v[:, :, DFF + j * 128:DFF + (j + 1) * 128],
                  writes=[f"scrgu{j}b"])

        def conv_dn(half, kg):
            nk = DN_GROUPS[kg]
            k0 = 4 * kg
            S.dma("pool", scr_dn[half, kg, :, 0:nk, :], w_dn_v[:, k0:k0 + nk, half * 512:(half + 1) * 512],
                  writes=[f"scrdn{half}_{kg}"])

        for j in range(NJ):
            conv_gu(j)
        for half in range(2):
            for kg in range(len(DN_GROUPS)):
                conv_dn(half, kg)

        ctx = {}

        def P0(gb):
            xs = gb % XR

            def post():
                if True:
                    S.task("act", lambda a: a.activation(out=xbr[:, 0, :], in_=xr[:, xs, :], func=AF.Copy),
                           reads=[f"x{xs}"], writes=["xb0"])
                else:
                    S.task("pool", lambda g: g.tensor_copy(out=xbr[:, 0, :], in_=xr[:, xs, :]),
                           reads=[f"x{xs}"], writes=["xb0"])
            return None, post

        def P1(gb):
            s2 = gb % 2
            c = {}
            return (lambda: tr_pe(lambda k: xbr[:, 0, k * 128:(k + 1) * 128], 8, ["xb0"], c),
                    lambda: tr_evac(8, xTr[:, s2, :, :], [f"xT{s2}"], c))

        def P2(gb):
            s2, s3 = gb % 2, gb % 3
            c = ctx.setdefault(gb, {})

            def pe():
                bp, bu, bv = bank(), bank(), bank()
                c["in"] = (bp, bu, bv)

                def f_in(t):
                    for k in range(8):
                        for n_, b_ in enumerate((bp, bu, bv)):
                            ins = t.matmul(ps[:, b_, :], lhsT=xTr[:, s2, k, :],
                                           rhs=win[:, k, n_ * 512:(n_ + 1) * 512], start=(k == 0), stop=(k == 7))
                    return ins
                S.task("pe", f_in, reads=[f"xT{s2}", "win"], excl=[bn(bp), bn(bu), bn(bv)])

            def post():
                bp, bu, bv = c["in"]
                S.task("act", lambda a: a.activation(out=gvr[:, s2, :], in_=ps[:, bv, :], func=AF.Gelu),
                       excl=[bn(bv)], writes=[f"gv{s2}"])
                S.task("act", lambda a: a.activation(out=gur[:, s2, :], in_=ps[:, bu, :], func=AF.Gelu),
                       excl=[bn(bu)], writes=[f"gu{s2}"])
                S.task("act", lambda a: a.activation(out=xpr[:, s3, :], in_=ps[:, bp, :], func=AF.Copy),
                       excl=[bn(bp)], writes=[f"xp{s3}"])
                c["lnv"] = ln_a(gvr[:, s2, :], 512, f"gv{s2}")
            return pe, post

        def P2b(gb):
            s2 = gb % 2
            c = ctx.setdefault(gb, {})
            return None, (lambda: ln_b(c["lnv"], gvr[:, s2, :], sgB[:], sbB[:], vnr[:, s2, :], f"gv{s2}", f"vn{s2}", gname="sgB", bname="sbB"))

        def P3(gb):
            s2, s3, sp3 = gb % 2, gb % 3, (gb - 1) % 3
            first = (gb % 16 == 0)
            pc, pcn = (pfirst, "pfirst") if first else (prest, "prest")
            c = ctx.setdefault(gb, {})

            def pe():
                bq = bank()
                c["bq"] = bq

                def f_pool(t):
                    for g in range(4):
                        ins = t.matmul(ps[:, bq, g * 128:(g + 1) * 128], lhsT=xpr[:, s3, g * 128:(g + 1) * 128],
                                       rhs=pc[:, g, :], start=True, stop=first)
                        if not first:
                            ins = t.matmul(ps[:, bq, g * 128:(g + 1) * 128],
                                           lhsT=xpr[:, sp3, g * 128:(g + 1) * 128],
                                           rhs=phalo[:, g, :], start=False, stop=True)
                    return ins
                rd = [f"xp{s3}", pcn] + ([] if first else [f"xp{sp3}", "phalo"])
                S.task("pe", f_pool, reads=rd, excl=[bn(bq)])

            def post():
                bq = c["bq"]
                S.task("act", lambda a: a.activation(out=pTr[:, s2, :, :].rearrange("p g t -> p (g t)"),
                                                     in_=ps[:, bq, :], func=AF.Copy),
                       excl=[bn(bq)], writes=[f"pT{s2}"])
            return pe, post

        def P4p(gb):
            s2 = gb % 2
            first = (gb % 16 == 0)
            sc, scn = (scf, "scf") if first else (scr_, "scr")
            c = ctx.setdefault(gb, {})

            def pe():
                br = bank()
                c["br"] = br

                def f_pw(t):
                    for g in range(4):
                        ins = t.matmul(ps[:, br, g * 128:(g + 1) * 128], lhsT=poolw[:, g, :], rhs=pTr[:, s2, g, :],
                                       start=True, stop=True)
                    return ins
                S.task("pe", f_pw, reads=[f"pT{s2}", "poolw"], excl=[bn(br)])

            def post():
                br = c["br"]
                S.task("dve", lambda v: v.tensor_tensor(out=mixr[:, s2, 0:4, :],
                                                        in0=ps[:, br, :].rearrange("p (g t) -> p g t", g=4),
                                                        in1=sc[:], op=ALU.mult),
                       reads=[scn], excl=[bn(br)], writes=[f"mixA{s2}"])
            return pe, post

        def P4s(gb):
            s2 = gb % 2
            c = ctx.setdefault(gb, {})

            def pe():
                bs_ = bank()
                c["bs"] = bs_

                def f_sgu(t):
                    for h in range(4):
                        ins = t.matmul(ps[:, bs_, h * 128:(h + 1) * 128], lhsT=wsT[:, h, :],
                                       rhs=vnr[:, s2, h * 128:(h + 1) * 128], start=True, stop=True)
                    return ins
                S.task("pe", f_sgu, reads=[f"vn{s2}", "wsT"], excl=[bn(bs_)])

            def post():
                bs_ = c["bs"]

                def f_gate(v):
                    for h in range(4):
                        ins = v.scalar_tensor_tensor(out=sgr[:, s2, h * 128:(h + 1) * 128],
                                                     in0=ps[:, bs_, h * 128:(h + 1) * 128], scalar=bsT[:, h:h + 1],
                                                     in1=gur[:, s2, h * 128:(h + 1) * 128],
                                                     op0=ALU.add, op1=ALU.mult)
                    return ins
                S.task("dve", f_gate, reads=[f"gu{s2}", "bsT"], excl=[bn(bs_)], writes=[f"sg{s2}"])
            return pe, post

        def P5(gb):
            s2 = gb % 2
            c = {}
            return (lambda: tr_pe(lambda k: sgr[:, s2, k * 128:(k + 1) * 128], 4, [f"sg{s2}"], c),
                    lambda: tr_evac(4, mixr[:, s2, 4:8, :], [f"mixB{s2}"], c, eng="act"))

        def P6(gb):
            s2, xs, hs = gb % 2, gb % XR, gb % H_RING
            c = ctx.setdefault(gb, {})

            def pe():
                b0, b1 = bank(), bank()
                c["out"] = (b0, b1)

                def f_out(t):
                    for k in range(8):
                        for n_, b_ in enumerate((b0, b1)):
                            ins = t.matmul(ps[:, b_, :], lhsT=mixr[:, s2, k, :],
                                           rhs=wout[:, k, n_ * 512:(n_ + 1) * 512], start=(k == 0), stop=(k == 7))
                    return ins
                S.task("pe", f_out, reads=[f"mixA{s2}", f"mixB{s2}", "wout"], excl=[bn(b0), bn(b1)])

            def post():
                for n_, b_ in enumerate(c["out"]):
                    S.task("dve", lambda v, n_=n_, b_=b_: v.scalar_tensor_tensor(
                        out=hr[:, hs, n_ * 512:(n_ + 1) * 512], in0=xr[:, xs, n_ * 512:(n_ + 1) * 512],
                        scalar=ALPHA, in1=ps[:, b_, :], op0=ALU.mult, op1=ALU.add),
                        reads=[f"x{xs}"], excl=[bn(b_)], writes=[f"h{hs}"])
                load_x(gb + XR)
                c["ln1"] = ln_a(hr[:, hs, :], D, f"h{hs}")
            return pe, post

        def P6b(gb):
            hs = gb % H_RING
            c = ctx.setdefault(gb, {})
            return None, (lambda: ln_b(c["ln1"], hr[:, hs, :], g1B[:], b1B[:], hr[:, hs, :], f"h{hs}", gname="g1B", bname="b1B"))

        def P6c(gb):
            hs, hb = gb % H_RING, gb % HB

            def post():
                if True:
                    S.task("act", lambda a: a.activation(out=hbr[:, hb, :], in_=hr[:, hs, :], func=AF.Copy),
                           reads=[f"h{hs}"], writes=[f"hb{hb}"])
                else:
                    S.task("pool", lambda g: g.tensor_copy(out=hbr[:, hb, :], in_=hr[:, hs, :]),
                           reads=[f"h{hs}"], writes=[f"hb{hb}"])
            return None, post

        def P7(gb):
            hb, pos = gb % HB, gb % 4
            c = {}
            return (lambda: tr_pe(lambda k: hbr[:, hb, k * 128:(k + 1) * 128], 8, [f"hb{hb}"], c),
                    lambda: tr_evac(8, hT[:, :, pos * 128:(pos + 1) * 128], [f"hT{pos}"], c))

        PSTEPS = (P0, P1, P2, P2b, P3, P4p, P4s, P5, P6, P6b, P6c)
        PBANKS = {P0: 0, P1: 1, P2: 3, P2b: 0, P3: 1, P4p: 1, P4s: 1, P5: 1, P6: 2, P6b: 0, P6c: 0, P7: 1}
        PCOST = {P0: 0.3, P1: 0.9, P2: 1.2, P2b: 1.8, P3: 0.3, P4p: 0.9, P4s: 1.3, P5: 0.3, P6: 3.6, P6b: 2.5,
                 P6c: 0.3, P7: 0.9}
        SKEW = 5

        def mk(fn, gb):
            pe, post = fn(gb)
            return (pe, post, PBANKS[fn], gb, False, PCOST[fn])

        def mixer_steps(g0):
            keyed = sorted([(max(0, k + SKEW * b - (3 if k == 0 else 0)), b, k)
                            for b in range(4) for k in range(len(PSTEPS))])
            return [mk(PSTEPS[k], g0 + b) for _, b, k in keyed]

        gu_issue = [0]
        dn_issue = [0]

        def issue_wgu(j):
            s = gu_issue[0] % WG
            gu_issue[0] += 1
            S.dma("sp", wgur[:, s, :, :], scr_gu[j], reads=[f"scrgu{j}a", f"scrgu{j}b"], writes=[f"wgu{s}"])
            return s

        def issue_wdn(half, kg):
            s = dn_issue[0] % WD
            dn_issue[0] += 1
            nk = DN_GROUPS[kg]
            S.dma("sp", wdnr[:, s, 0:nk, :], scr_dn[half, kg, :, 0:nk, :], reads=[f"scrdn{half}_{kg}"],
                  writes=[f"wdn{s}"])
            return s

        sil_ctr = [0]
        out_toks = []

        def gate_up_chunk(j, s):
            c = {}

            def pe():
                bg, bu = bank(), bank()
                c["b"] = (bg, bu)

                def f_gu(t):
                    for k in range(8):
                        ins = t.matmul(ps[:, bg, :], lhsT=wgur[:, s, k, 0:128], rhs=hT[:, k, :],
                                       start=(k == 0), stop=(k == 7))
                    for k in range(8):
                        ins = t.matmul(ps[:, bu, :], lhsT=wgur[:, s, k, 128:256], rhs=hT[:, k, :],
                                       start=(k == 0), stop=(k == 7))
                    return ins
                S.task("pe", f_gu, reads=[f"wgu{s}", "hT0", "hT1", "hT2", "hT3"], excl=[bn(bg), bn(bu)])

            def post():
                bg, bu = c["b"]
                q = sil_ctr[0] % SIL
                sil_ctr[0] += 1
                S.task("act", lambda a: a.activation(out=silr[:, q, :], in_=ps[:, bg, :], func=AF.Tanh, scale=0.5),
                       excl=[bn(bg)], writes=[f"sil{q}"])
                S.task("dve", lambda v: v.scalar_tensor_tensor(out=silr[:, q, :], in0=silr[:, q, :], scalar=1.0,
                                                               in1=ps[:, bg, :], op0=ALU.add, op1=ALU.mult),
                       reads=[f"sil{q}"], excl=[bn(bg)], writes=[f"sil{q}"])
                S.task("dve", lambda v: v.scalar_tensor_tensor(out=aT[:, j, :], in0=silr[:, q, :], scalar=0.5,
                                                               in1=ps[:, bu, :], op0=ALU.mult, op1=ALU.mult),
                       reads=[f"sil{q}"], excl=[bn(bu)], writes=[f"aT{j}"])
            return pe, post, 2, None, False, 0.0

        def tail_steps(mt):
            steps = []
            stores = []
            slot = {}
            for b in range(4):
                gb = mt * 4 + b
                hs = gb % H_RING

                def a(gb=gb, hs=hs):
                    slot[gb] = ln_a(hr[:, hs, :], D, f"h{hs}")

                def b_(gb=gb, hs=hs):
                    ln_b(slot[gb], hr[:, hs, :], g2B[:], b2B[:], hr[:, hs, :], f"h{hs}", gname="g2B", bname="b2B")

                def c_(gb=gb, hs=hs):
                    out_toks.append(S.dma("sp", out_d[gb * 128:(gb + 1) * 128, :], hr[:, hs, :],
                                          reads=[f"h{hs}"]))
                steps.append([(None, a, 0, None, False, 1.8), (None, b_, 0, None, False, 2.5)])
                stores.append((None, c_, 0, None, False, 0.3))
            keyed = sorted([(k + b, b, k) for b in range(4) for k in range(2)])
            return [steps[b][k] for _, b, k in keyed], stores

        def run_slot(parts):
            flush_sqrt()
            pending = []
            used = 0

            def flush():
                nonlocal used
                for p in pending:
                    p()
                pending.clear()
                used = 0
            tags = set()
            pst_busy = False
            for pe, post, nb, tag, uses_pst, _cost in parts:
                if pe is not None:
                    if used + nb > len(free_banks) or (tag is not None and tag in tags) or (uses_pst and pst_busy):
                        flush()
                        tags.clear()
                        pst_busy = False
                    pe()
                    used += nb
                    pst_busy = pst_busy or uses_pst
                if post is not None:
                    pending.append(post)
                    if tag is not None:
                        tags.add(tag)
            flush()

        pre_gu = {}

        def ffn(mt, fill, n_fill_slots):
            PF = WG - 1
            NG = len(DN_GROUPS)
            PFD = WD - 1
            chunks = [(half, kg) for half in range(2) for kg in range(NG)]
            fi = 0

            cap = [1.0] * NJ + [1.7 * DN_GROUPS[kg_] / 4.0 for _, kg_ in chunks]
            cap = [c_ if i < n_fill_slots else 0.0 for i, c_ in enumerate(cap)]
            cap_tot = sum(cap)
            cost_tot = sum(f_[5] for f_ in fill)
            cap_cum = [0.0]
            taken_cost = [0.0]

            def take(slot_idx):
                nonlocal fi
                cap_cum[0] += cap[slot_idx]
                tgt_cost = cost_tot * cap_cum[0] / cap_tot
                out = []
                last = slot_idx >= n_fill_slots - 1
                while fi < len(fill) and (last or taken_cost[0] + 0.5 * fill[fi][5] <= tgt_cost):
                    taken_cost[0] += fill[fi][5]
                    out.append(fill[fi])
                    fi += 1
                return out

            slots = dict(pre_gu.pop(mt, {}))
            for j in range(PF):
                if j not in slots:
                    slots[j] = issue_wgu(j)
            dslots = {}
            for j in range(NJ):
                if j + PF < NJ:
                    slots[j + PF] = issue_wgu(j + PF)
                if j == NJ - 2:
                    for c in range(PFD):
                        dslots[c] = issue_wdn(*chunks[c])
                run_slot(take(j) + [gate_up_chunk(j, slots[j])])
            bd = None
            for c, (half, kg) in enumerate(chunks):
                if c + PFD < len(chunks):
                    dslots[c + PFD] = issue_wdn(*chunks[c + PFD])
                if c == len(chunks) - 3 and mt + 1 < NMT:
                    pre_gu[mt + 1] = {j: issue_wgu(j) for j in range(PF)}
                if kg == 0:
                    bd = [bank_hold() for _ in range(4)]
                s = dslots[c]
                nk = DN_GROUPS[kg]

                def pe(s=s, kg=kg, bd=bd, nk=nk):
                    def f_dn(t):
                        for kk in range(nk):
                            k = 4 * kg + kk
                            for b in range(4):
                                ins = t.matmul(ps[:, bd[b], :], lhsT=aT[:, k, b * 128:(b + 1) * 128],
                                               rhs=wdnr[:, s, kk, :], start=(k == 0), stop=(k == NJ - 1))
                        return ins
                    S.task("pe", f_dn, reads=[f"wdn{s}"] + [f"aT{4 * kg + kk}" for kk in range(nk)],
                           excl=[bn(b_) for b_ in bd])

                post = None
                if kg == NG - 1:
                    def post(bd=bd, half=half):
                        for b in range(4):
                            gb = mt * 4 + b
                            hs = gb % H_RING
                            S.task("dve", lambda v, hs=hs, b_=bd[b]: v.scalar_tensor_tensor(
                                out=hr[:, hs, half * 512:(half + 1) * 512],
                                in0=hr[:, hs, half * 512:(half + 1) * 512],
                                scalar=ALPHA, in1=ps[:, b_, :], op0=ALU.mult, op1=ALU.add),
                                reads=[f"h{hs}"], excl=[bn(bd[b])], writes=[f"h{hs}"])
                            bank_release(bd[b])
                run_slot(take(NJ + c) + [(pe, post, 0, None, False, 0.0)])
            assert fi == len(fill), (fi, len(fill))

        for i, st_ in enumerate(mixer_steps(0)):
            if i == 0:
                late_prologue()
            run_slot([st_])
        for b in range(4):
            run_slot([mk(P7, b)])
        tail, stores = [], []
        NSLOTS = NJ + 2 * len(DN_GROUPS)
        for mt in range(NMT):
            fill = []
            if mt + 1 < NMT:
                g1 = (mt + 1) * 4
                ms = mixer_steps(g1)
                fill += ms[:2]
                ms = ms[2:]
            fill += tail
            if mt + 1 < NMT:
                for i, st_ in enumerate(stores):
                    ms.insert(6 + 4 * i, st_)
                fill += ms
                fill += [mk(P7, g1 + b) for b in range(4)]
            else:
                fill += stores
            ffn(mt, fill, NSLOTS - 3)
            tail, stores = tail_steps(mt)
        for st_ in tail + stores:
            run_slot([st_])
        S.final_wait("sp", out_toks)

        with nc.Block() as block:
            @block.sync
            def _(e):
                S.run("sp", e)

            @block.scalar
            def _(e):
                S.run("act", e)

            @block.vector
            def _(e):
                S.run("dve", e)

            @block.gpsimd
            def _(e):
                S.run("pool", e)

            @block.tensor
            def _(e):
                S.run("pe", e)
    return nc


def _constants():
    c = {}
    c["c_ident"] = np.eye(128, dtype=np.float32)
    s = np.arange(128)[:, None]
    t = np.arange(128)[None, :]
    tril_ts = (np.arange(128)[None, :] <= np.arange(128)[:, None]).astype(np.float32)
    c["c_tril"] = np.ascontiguousarray(np.broadcast_to(tril_ts[:, None, :], (128, 4, 128))).astype(np.float32)
    pfirst = np.zeros((128, 4, 128), np.float32)
    prest = np.zeros((128, 4, 128), np.float32)
    phalo = np.zeros((128, 4, 128), np.float32)
    icf = np.zeros((128, 4, 128), np.float32)
    icr = np.zeros((128, 4, 128), np.float32)
    for g, w in enumerate(WINDOWS):
        band = ((s <= t) & (s > t - w)).astype(np.float32)
        cnt_first = np.minimum(np.arange(128) + 1, w).astype(np.float32)
        eye = (s == t).astype(np.float32)
        pfirst[:, g, :] = band - eye * cnt_first[None, :]
        prest[:, g, :] = band - eye * float(w)
        phalo[:, g, :] = (s >= 129 + t - w).astype(np.float32)
        icf[:, g, :] = (1.0 / cnt_first)[None, :]
        icr[:, g, :] = 1.0 / w
    c["c_pfirst"], c["c_prest"], c["c_phalo"] = pfirst, prest, phalo
    c["c_icfirst"], c["c_icrest"] = icf, icr
    return c


def _bcast(v, n=128):
    v = np.asarray(v, dtype=np.float32).reshape(1, -1)
    return np.ascontiguousarray(np.broadcast_to(v, (n, v.shape[1])))


_NC_CACHE = {}


def kernel(x, w_in, pool_w, pool_scale, sgu_ln_g, sgu_ln_b, sgu_w, sgu_b,
           w_out, ln1_g, ln1_b, w_gate_up, w_down, ln2_g, ln2_b):
    f = lambda a: np.ascontiguousarray(np.asarray(a, dtype=np.float32))
    x = f(x)
    B, Sq, Dm = x.shape
    assert (B, Sq, Dm) == (32, SEQ, D)
    shared = {
        "w_in": f(w_in)[0], "pool_w": f(pool_w)[0],
        "scaleB": np.ascontiguousarray(np.broadcast_to(
            f(pool_scale)[0].reshape(4, 128).T[:, :, None], (128, 4, 128))),
        "sgu_gB": _bcast(f(sgu_ln_g)[0]), "sgu_bB": _bcast(f(sgu_ln_b)[0]),
        "sgu_w": f(sgu_w)[0], "sgu_bT": np.ascontiguousarray(f(sgu_b)[0].T),
        "w_out": f(w_out)[0], "ln1_gB": _bcast(f(ln1_g)[0]), "ln1_bB": _bcast(f(ln1_b)[0]),
        "w_gate_up": f(w_gate_up)[0], "w_down": f(w_down)[0],
        "ln2_gB": _bcast(f(ln2_g)[0]), "ln2_bB": _bcast(f(ln2_b)[0]),
    }
    shared.update(_constants())
    xs = x.reshape(N_CORES, TOK, D)
    in_maps = [dict(shared, x=np.ascontiguousarray(xs[c])) for c in range(N_CORES)]
    if "nc" not in _NC_CACHE:
        _NC_CACHE["nc"] = build_nc()
    res = run_bass_kernel_spmd(_NC_CACHE["nc"], in_maps, core_ids=list(range(N_CORES)))
    out = np.stack([np.asarray(r["out"], dtype=np.float32) for r in res.results], axis=0)
    return out.reshape(B, Sq, Dm)
```
